# Optimizing a Trainium2 kernel written in Bass

```python
import math
import jax, jax.numpy as jnp
from jax import lax
import numpy as np

D_MODEL = 1024
BATCH = 1
SEQ = 16384
DEPTH = 2

CTX_LEN = 256
GRID_W = 64
MIX = 2 * D_MODEL
CHUNK = 64
EPS = 1e-6

SSD_INNER = MIX // 2
SSD_HEADDIM = 64
SSD_HEADS = SSD_INNER // SSD_HEADDIM
SSD_GROUPS = 2
SSD_STATE = 128
SSD_CONV = 5
SSD_CONV_DIM = SSD_INNER + 2 * SSD_GROUPS * SSD_STATE

GLA_WIDTH = MIX // 4
GLA_HEADS = 4
GLA_KEY = GLA_WIDTH // 2
GLA_HEAD_K = GLA_KEY // GLA_HEADS
GLA_HEAD_V = GLA_WIDTH // GLA_HEADS
GLA_RANK = 16
GLA_NORMALIZER = 16.0

HG_WIDTH = MIX // 4
HG_HEADS = 4
HG_EXPAND = 128
HG_KEY = HG_HEADS * HG_EXPAND
HG_HEAD_V = HG_WIDTH // HG_HEADS

FF = 4 * D_MODEL
N_MOD = 6 * D_MODEL

IN_SIZES = (SSD_INNER, SSD_CONV_DIM, 2 * SSD_HEADS,
            GLA_KEY, GLA_KEY, GLA_WIDTH, GLA_WIDTH, 2 * GLA_RANK,
            HG_KEY, 2 * HG_KEY, HG_WIDTH, HG_WIDTH)
N_IN = sum(IN_SIZES)

kernel_name = "hybrid_ssd_gla_hgrn2_prefix_dit"


def split_points():
    return np.cumsum(IN_SIZES)[:-1].tolist()


def rmsnorm(x, g):
    xf = x.astype(jnp.float32)
    y = xf * lax.rsqrt(jnp.mean(xf * xf, axis=-1, keepdims=True) + EPS)
    return (y * g.astype(jnp.float32)).astype(x.dtype)


def modulate(h, shift, scale):
    return h * (1 + scale) + shift


def centred_dwconv(u, w, bias, rows):
    b, l, ch = u.shape
    if rows is not None:
        u = u.reshape(b * rows, GRID_W, ch)
    pad = w.shape[0] // 2
    y = lax.conv_general_dilated(u, w[:, None, :].astype(u.dtype), (1,), [(pad, pad)],
                                 dimension_numbers=("NWC", "WIO", "NWC"), feature_group_count=ch)
    return y.reshape(b, l, ch) + bias


def segsum_exp(a_cs):
    n = a_cs.shape[-1]
    mask = jnp.tril(jnp.ones((n, n), bool))
    return jnp.exp(jnp.where(mask, a_cs[..., :, None] - a_cs[..., None, :], -jnp.inf))


def ssd_scan(xdt, a, bm, cm, s0):
    b, l, h, p = xdt.shape
    g, n = bm.shape[-2:]
    j = h // g
    c = l // CHUNK
    X = xdt.astype(jnp.float32).reshape(b, c, CHUNK, g, j, p)
    A = a.astype(jnp.float32).reshape(b, c, CHUNK, g, j).transpose(0, 3, 4, 1, 2)
    Bc = bm.astype(jnp.float32).reshape(b, c, CHUNK, g, n)
    Cc = cm.astype(jnp.float32).reshape(b, c, CHUNK, g, n)
    a_cs = jnp.cumsum(A, axis=-1)
    scores = jnp.einsum("bclgn,bcsgn->bgcls", Cc, Bc)
    y_diag = jnp.einsum("bgcls,bgjcls,bcsgjp->bclgjp", scores, segsum_exp(a_cs), X)
    decay_states = jnp.exp(a_cs[..., -1:] - a_cs)
    states = jnp.einsum("bcsgn,bgjcs,bcsgjp->cbgjpn", Bc, decay_states, X)
    chunk_decay = jnp.exp(a_cs[..., -1]).transpose(3, 0, 1, 2)

    def step(S, inp):
        dec, st = inp
        return dec[..., None, None] * S + st, S

    s_final, s_in = lax.scan(step, s0.astype(jnp.float32).reshape(b, g, j, p, n), (chunk_decay, states))
    y_off = jnp.einsum("bclgn,bgjcl,cbgjpn->bclgjp", Cc, jnp.exp(a_cs), s_in)
    return (y_diag + y_off).reshape(b, l, h, p), s_final.reshape(b, h, p, n)


def gla_scan(q, k, v, log_a, s0):
    b, l, h, dk = q.shape
    dv = v.shape[-1]
    c = l // CHUNK

    def to_chunks(t):
        return t.astype(jnp.float32).reshape(b, c, CHUNK, h, t.shape[-1]).transpose(1, 0, 3, 2, 4)

    mask = jnp.tril(jnp.ones((CHUNK, CHUNK), bool))[:, :, None]

    def step(S, inp):
        qc, kc, vc, gc = inp
        bcs = jnp.cumsum(gc, axis=-2)
        inter = jnp.einsum("bhld,bhde->bhle", qc * jnp.exp(bcs), S)
        pair = jnp.exp(jnp.where(mask, bcs[:, :, :, None, :] - bcs[:, :, None, :, :], -jnp.inf))
        att = jnp.einsum("bhld,bhsd,bhlsd->bhls", qc, kc, pair)
        intra = jnp.einsum("bhls,bhse->bhle", att, vc)
        last = bcs[:, :, -1:, :]
        S = jnp.exp(last[:, :, 0, :])[..., None] * S + jnp.einsum("bhsd,bhse->bhde", kc * jnp.exp(last - bcs), vc)
        return S, inter + intra

    s_final, o = lax.scan(step, s0.astype(jnp.float32), (to_chunks(q), to_chunks(k), to_chunks(v), to_chunks(log_a)))
    return o.transpose(1, 0, 3, 2, 4).reshape(b, l, h, dv), s_final


def bidir_prefix_scan(scan_fn, ctx_dirs, lat_dirs, s0):
    flip = lambda t: jnp.flip(t, axis=1)
    yc_f, sc_f = scan_fn(*ctx_dirs[0], s0)
    yl_f, _ = scan_fn(*lat_dirs[0], sc_f)
    yc_b, sc_b = scan_fn(*[flip(t) for t in ctx_dirs[1]], s0)
    yl_b, _ = scan_fn(*[flip(t) for t in lat_dirs[1]], sc_b)
    return yc_f + flip(yc_b), yl_f + flip(yl_b)


def mixer_inputs(h, w_in, conv_w, conv_b, dt_bias, a_log, gla_w_gk2, gla_b_gk, lb, rows):
    f32 = jnp.float32
    b, l, _ = h.shape
    z, xbc, dt, gq, gk, gv, gg, glr, hq, hf, hi, hgate = jnp.split(h @ w_in, split_points(), axis=-1)
    xbc = jax.nn.silu(centred_dwconv(xbc, conv_w, conv_b, rows))
    xs, bm, cm = jnp.split(xbc, [SSD_INNER, SSD_INNER + SSD_GROUPS * SSD_STATE], axis=-1)
    xs = xs.reshape(b, l, SSD_HEADS, SSD_HEADDIM).astype(f32)
    bm = bm.reshape(b, l, SSD_GROUPS, SSD_STATE)
    cm = cm.reshape(b, l, SSD_GROUPS, SSD_STATE)
    dt = jax.nn.softplus(dt.reshape(b, l, 2, SSD_HEADS).astype(f32) + dt_bias.astype(f32))
    a = -jnp.exp(a_log.astype(f32))
    ssd_dirs = tuple((xs * dt[:, :, d, :, None], dt[:, :, d] * a[d], bm, cm) for d in range(2))
    q = gq.reshape(b, l, GLA_HEADS, GLA_HEAD_K).astype(f32) * GLA_HEAD_K ** -0.5
    k = gk.reshape(b, l, GLA_HEADS, GLA_HEAD_K).astype(f32)
    v = gv.reshape(b, l, GLA_HEADS, GLA_HEAD_V).astype(f32)
    gate_logit = jnp.einsum("bldr,drk->bldk", glr.reshape(b, l, 2, GLA_RANK), gla_w_gk2) + gla_b_gk
    log_a = (jax.nn.log_sigmoid(gate_logit.astype(f32)) / GLA_NORMALIZER).reshape(b, l, 2, GLA_HEADS, GLA_HEAD_K)
    gla_dirs = tuple((q, k, v, log_a[:, :, d]) for d in range(2))
    lbh = lb.astype(f32).reshape(HG_HEADS, HG_EXPAND)
    hq = jax.nn.silu(hq.astype(f32)).reshape(b, l, HG_HEADS, HG_EXPAND)
    hf = hf.reshape(b, l, 2, HG_HEADS, HG_EXPAND).astype(f32)
    log_f = jnp.logaddexp(jnp.log(lbh), jnp.log1p(-lbh) + jax.nn.log_sigmoid(hf))
    k_hg = (1.0 - lbh) * jax.nn.sigmoid(-hf)
    hi = hi.reshape(b, l, HG_HEADS, HG_HEAD_V).astype(f32)
    hg_dirs = tuple((hq, k_hg[:, :, d], hi, log_f[:, :, d]) for d in range(2))
    return (ssd_dirs, gla_dirs, hg_dirs), (xs, z, gg, hgate)


def mixer_output(y_ssd, y_gla, y_hg, post, d_skip, ssd_norm_g, gla_norm_g, hg_norm_g, w_out, dtype):
    f32 = jnp.float32
    xs, z, gg, hgate = post
    b, l = z.shape[:2]
    y_s = (y_ssd + d_skip.astype(f32)[:, None] * xs).reshape(b, l, SSD_INNER)
    y_s = rmsnorm(y_s * jax.nn.silu(z.astype(f32)), ssd_norm_g)
    y_g = rmsnorm(y_gla, gla_norm_g).reshape(b, l, GLA_WIDTH) * jax.nn.silu(gg.astype(f32))
    y_h = rmsnorm(y_hg, hg_norm_g).reshape(b, l, HG_WIDTH) * jax.nn.sigmoid(hgate.astype(f32))
    return jnp.concatenate([y_s, y_g, y_h], axis=-1).astype(dtype) @ w_out


def hybrid_mixer(hc, hl, w_in, conv_w, conv_b, dt_bias, a_log, d_skip, ssd_norm_g, gla_w_gk2, gla_b_gk,
                 gla_norm_g, lb, hg_norm_g, w_out, rows, with_ctx):
    b = hl.shape[0]
    ctx_dirs, ctx_post = mixer_inputs(hc, w_in, conv_w, conv_b, dt_bias, a_log, gla_w_gk2, gla_b_gk, lb, None)
    lat_dirs, lat_post = mixer_inputs(hl, w_in, conv_w, conv_b, dt_bias, a_log, gla_w_gk2, gla_b_gk, lb, rows)
    zeros = lambda *s: jnp.zeros((b,) + s, jnp.float32)
    yc_s, yl_s = bidir_prefix_scan(ssd_scan, ctx_dirs[0], lat_dirs[0], zeros(SSD_HEADS, SSD_HEADDIM, SSD_STATE))
    yc_g, yl_g = bidir_prefix_scan(gla_scan, ctx_dirs[1], lat_dirs[1], zeros(GLA_HEADS, GLA_HEAD_K, GLA_HEAD_V))
    yc_h, yl_h = bidir_prefix_scan(gla_scan, ctx_dirs[2], lat_dirs[2], zeros(HG_HEADS, HG_EXPAND, HG_HEAD_V))
    y_lat = mixer_output(yl_s, yl_g, yl_h, lat_post, d_skip, ssd_norm_g, gla_norm_g, hg_norm_g, w_out, hl.dtype)
    y_ctx = None
    if with_ctx:
        y_ctx = mixer_output(yc_s, yc_g, yc_h, ctx_post, d_skip, ssd_norm_g, gla_norm_g, hg_norm_g, w_out, hc.dtype)
    return y_ctx, y_lat


def sq_relu_mlp(h, w1, w2):
    return jnp.square(jax.nn.relu(h @ w1)) @ w2


def setup_inputs(seed: int = 0) -> dict:
    key = jax.random.key(seed)
    ks = jax.random.split(key, 32)
    f32 = jnp.float32
    nrm = lambda k, shape, scale: jax.random.normal(k, shape, f32) * scale
    L = DEPTH
    dt0 = jnp.exp(jax.random.uniform(ks[10], (L, 2, SSD_HEADS), f32, math.log(1e-3), math.log(1e-1)))
    return {
        "x": nrm(ks[0], (BATCH, SEQ, D_MODEL), 1.0),
        "c": nrm(ks[1], (BATCH, D_MODEL), 1.0),
        "ctx": nrm(ks[2], (BATCH, CTX_LEN, D_MODEL), 1.0),
        "c_ctx": nrm(ks[3], (D_MODEL,), 1.0),
        "norm1_g": 1.0 + nrm(ks[4], (L, D_MODEL), 0.02),
        "norm2_g": 1.0 + nrm(ks[5], (L, D_MODEL), 0.02),
        "w_mod": nrm(ks[6], (L, D_MODEL, N_MOD), 0.3 * D_MODEL ** -0.5),
        "b_mod": nrm(ks[7], (L, N_MOD), 0.01),
        "w_in": nrm(ks[8], (L, D_MODEL, N_IN), D_MODEL ** -0.5),
        "ssd_conv_w": nrm(ks[9], (L, SSD_CONV, SSD_CONV_DIM), SSD_CONV ** -0.5),
        "ssd_conv_b": nrm(ks[11], (L, SSD_CONV_DIM), 0.01),
        "ssd_dt_bias": dt0 + jnp.log(-jnp.expm1(-dt0)),
        "ssd_a_log": jnp.log(jax.random.uniform(ks[12], (L, 2, SSD_HEADS), f32, 1.0, 16.0)),
        "ssd_d": 1.0 + nrm(ks[13], (L, SSD_HEADS), 0.02),
        "ssd_norm_g": 1.0 + nrm(ks[14], (L, SSD_INNER), 0.02),
        "gla_w_gk2": nrm(ks[15], (L, 2, GLA_RANK, GLA_KEY), GLA_RANK ** -0.5),
        "gla_b_gk": nrm(ks[16], (L, 2, GLA_KEY), 0.01),
        "gla_norm_g": 1.0 + nrm(ks[17], (L, GLA_HEAD_V), 0.02),
        "hg_lb_logits": nrm(ks[18], (L, HG_KEY), 0.5),
        "hg_norm_g": 1.0 + nrm(ks[19], (L, HG_HEAD_V), 0.02),
        "w_out": nrm(ks[20], (L, MIX, D_MODEL), MIX ** -0.5),
        "w_mlp1": nrm(ks[21], (L, D_MODEL, FF), D_MODEL ** -0.5),
        "w_mlp2": nrm(ks[22], (L, FF, D_MODEL), FF ** -0.5),
        "final_norm_g": 1.0 + nrm(ks[23], (D_MODEL,), 0.02),
    }


def reference(x, c, ctx, c_ctx, norm1_g, norm2_g, w_mod, b_mod, w_in, ssd_conv_w, ssd_conv_b, ssd_dt_bias,
              ssd_a_log, ssd_d, ssd_norm_g, gla_w_gk2, gla_b_gk, gla_norm_g, hg_lb_logits, hg_norm_g, w_out,
              w_mlp1, w_mlp2, final_norm_g):
    seq = x.shape[1]
    rows = seq // GRID_W
    lbs = jnp.cumsum(jax.nn.softmax(hg_lb_logits.astype(jnp.float32), axis=0), axis=0)
    lbs = lbs - lbs[0]
    h_ctx = ctx
    for layer in range(DEPTH):
        last = layer == DEPTH - 1
        mod_lat = jax.nn.silu(c) @ w_mod[layer] + b_mod[layer]
        mod_ctx = jax.nn.silu(c_ctx) @ w_mod[layer] + b_mod[layer]
        sh1, sc1, g1, sh2, sc2, g2 = jnp.split(mod_lat[:, None, :], 6, axis=-1)
        csh1, csc1, cg1, csh2, csc2, cg2 = jnp.split(mod_ctx, 6, axis=-1)
        hl = modulate(rmsnorm(x, norm1_g[layer]), sh1, sc1)
        hc = modulate(rmsnorm(h_ctx, norm1_g[layer]), csh1, csc1)
        y_ctx, y_lat = hybrid_mixer(hc, hl, w_in[layer], ssd_conv_w[layer], ssd_conv_b[layer], ssd_dt_bias[layer],
                                    ssd_a_log[layer], ssd_d[layer], ssd_norm_g[layer], gla_w_gk2[layer],
                                    gla_b_gk[layer], gla_norm_g[layer], lbs[layer], hg_norm_g[layer], w_out[layer],
                                    rows, not last)
        x = x + g1 * y_lat
        x = x + g2 * sq_relu_mlp(modulate(rmsnorm(x, norm2_g[layer]), sh2, sc2), w_mlp1[layer], w_mlp2[layer])
        if not last:
            h_ctx = h_ctx + cg1 * y_ctx
            h_ctx = h_ctx + cg2 * sq_relu_mlp(modulate(rmsnorm(h_ctx, norm2_g[layer]), csh2, csc2),
                                              w_mlp1[layer], w_mlp2[layer])
    return rmsnorm(x, final_norm_g)
```

```python
import numpy as np
import concourse.bass as bass
import concourse.mybir as mybir
from concourse.bass_utils import run_bass_kernel_spmd

F32 = mybir.dt.float32
BF16 = mybir.dt.bfloat16
I32 = mybir.dt.int32
AF = mybir.ActivationFunctionType
ALU = mybir.AluOpType


class Res:
    __slots__ = ("name", "w", "r", "excl")

    def __init__(self, name, excl=False):
        self.name = name
        self.w = []
        self.r = []
        self.excl = excl


class V:
    __slots__ = ("ap", "res")

    def __init__(self, ap, res):
        self.ap = ap
        self.res = res

    def __getitem__(self, idx):
        return V(self.ap[idx], self.res)

    def bcast(self, shape):
        return V(self.ap.broadcast_to(list(shape)), self.res)

    def rr(self, pat, **kw):
        return V(self.ap.rearrange(pat, **kw), self.res)

    def bitcast(self, dt):
        return V(self.ap.bitcast(dt), self.res)

    @property
    def shape(self):
        return self.ap.shape


def _ap(x):
    return x.ap if isinstance(x, V) else x


class Sched:
    ENGS = ("pe", "act", "dve", "pool", "sp")

    def __init__(self, nc, n_dma_sems=40):
        self.nc = nc
        self.eng_obj = {"pe": nc.tensor, "act": nc.scalar, "dve": nc.vector, "pool": nc.gpsimd, "sp": nc.sync}
        self.items = {e: [] for e in self.ENGS}
        self.esem = {e: nc.alloc_semaphore("es_" + e) for e in self.ENGS}
        self.ecnt = {e: 0 for e in self.ENGS}
        self.seen = {e: {} for e in self.ENGS}
        self.dsems = [nc.alloc_semaphore("ds%d" % i) for i in range(n_dma_sems)]
        self.dcnt = [0] * n_dma_sems
        self.di = 0
        self.nuid = 0
        self.stacks = []

    def sb(self, name, shape, dt):
        self.nuid += 1
        nm = "%s_%d" % (name, self.nuid)
        if self.stacks:
            t = self.stacks[-1].enter_context(self.nc.sbuf_tensor(nm, list(shape), dt))
        else:
            t = self.nc.alloc_sbuf_tensor(nm, list(shape), dt)
        return V(t.ap(), Res(name))

    def scope(self):
        return _Scope(self)

    def barrier(self):
        for e in self.ENGS:
            waits = []
            for x in self.ENGS:
                if x != e and self.ecnt[x] > 0 and self.seen[e].get(id(self.esem[x]), 0) < self.ecnt[x]:
                    waits.append((self.esem[x], self.ecnt[x]))
                    self.seen[e][id(self.esem[x])] = self.ecnt[x]
            for j, sem in enumerate(self.dsems):
                if self.dcnt[j] > 0 and self.seen[e].get(id(sem), 0) < self.dcnt[j]:
                    waits.append((sem, self.dcnt[j]))
                    self.seen[e][id(sem)] = self.dcnt[j]
            if waits:
                self.items[e].append((waits, None, None, 0))

    def ps(self, name, shape, dt):
        self.nuid += 1
        t = self.nc.alloc_psum_tensor("%s_%d" % (name, self.nuid), list(shape), dt)
        return V(t.ap(), Res(name, excl=True))

    def dram(self, name, shape, dt, kind):
        t = self.nc.dram_tensor(name, list(shape), dt, kind=kind)
        return V(t.ap(), Res(name))

    def _collect(self, eng, reads, writes, skip_same=False):
        waits = {}
        def add(ev):
            s, v = ev
            if skip_same and s is self.esem[eng]:
                return
            k = id(s)
            if self.seen[eng].get(k, 0) >= v:
                return
            if k not in waits or waits[k][1] < v:
                waits[k] = (s, v)
        for r in reads:
            if r is None:
                continue
            for ev in r.res.w:
                add(ev)
        for w in writes:
            for ev in w.res.w:
                add(ev)
            for ev in w.res.r:
                add(ev)
        for k, (s, v) in waits.items():
            self.seen[eng][k] = v
        return list(waits.values())

    def _commit(self, ev, reads, writes):
        wset = set(id(w.res) for w in writes)
        for w in writes:
            w.res.w = [ev]
            w.res.r = []
        for r in reads:
            if r is None or id(r.res) in wset:
                continue
            lst = [e for e in r.res.r if e[0] is not ev[0]]
            lst.append(ev)
            r.res.r = lst

    def op(self, eng, fn, reads=(), writes=()):
        reads = [r for r in reads if isinstance(r, V)]
        writes = list(writes) + [r for r in reads if r.res.excl and eng != "pe"]
        waits = self._collect(eng, reads, writes, skip_same=(eng == "pe"))
        self.ecnt[eng] += 1
        ev = (self.esem[eng], self.ecnt[eng])
        self.items[eng].append((waits, fn, self.esem[eng], 1))
        self._commit(ev, reads, writes)

    def dma(self, q, out, in_, **kw):
        j = self.di
        self.di = (self.di + 1) % len(self.dsems)
        sem = self.dsems[j]
        waits = self._collect(q, [in_], [out])
        prev = self.dcnt[j]
        if prev > 0 and self.seen[q].get(id(sem), 0) < prev:
            waits.append((sem, prev))
            self.seen[q][id(sem)] = prev
        self.dcnt[j] += 16
        ev = (sem, self.dcnt[j])
        o_ap, i_ap = _ap(out), _ap(in_)
        self.items[q].append((waits, lambda e: e.dma_start(out=o_ap, in_=i_ap, **kw), sem, 16))
        self._commit(ev, [in_], [out])

    def coll(self, src, dst, groups):
        sem = self.nc.alloc_semaphore("cs%d" % self.nuid)
        self.nuid += 1
        waits = self._collect("pool", [src], [dst])
        s_ap, d_ap = _ap(src), _ap(dst)
        self.items["pool"].append((waits, lambda e: e.collective_compute("AllGather", ALU.bypass, groups, [s_ap], [d_ap]), sem, 16))
        self._commit((sem, 16), [src], [dst])

    def mm(self, out, lhsT, rhs, start=True, stop=True):
        o, l, r = _ap(out), _ap(lhsT), _ap(rhs)
        self.op("pe", lambda e: e.matmul(o, l, r, start=start, stop=stop), [lhsT, rhs], [out])

    def tr(self, out, in_, ident):
        o, i, d = _ap(out), _ap(in_), _ap(ident)
        self.op("pe", lambda e: e.transpose(o, i, d), [in_, ident], [out])

    def act(self, out, in_, func, bias=None, scale=None, accum=None, eng="act"):
        o, i = _ap(out), _ap(in_)
        kw = {}
        if bias is not None:
            kw["bias"] = _ap(bias)
        if scale is not None:
            kw["scale"] = _ap(scale)
        if accum is not None:
            kw["accum_out"] = _ap(accum)
        wr = [out] + ([accum] if accum is not None else [])
        self.op("act", lambda e: e.activation(o, i, func, **kw), [in_, bias, scale], wr)

    def tt(self, out, a, b, op, eng="dve"):
        o, x, y = _ap(out), _ap(a), _ap(b)
        self.op(eng, lambda e: e.tensor_tensor(o, x, y, op), [a, b], [out])

    def ts(self, out, a, s1, op0, s2=None, op1=None, eng="dve", accum=None):
        o, x = _ap(out), _ap(a)
        a1, a2 = _ap(s1), _ap(s2)
        kw = {}
        if accum is not None:
            kw["accum_out"] = _ap(accum)
        wr = [out] + ([accum] if accum is not None else [])
        if op1 is None:
            self.op(eng, lambda e: e.tensor_scalar(o, x, a1, None, op0, **kw), [a, s1], wr)
        else:
            self.op(eng, lambda e: e.tensor_scalar(o, x, a1, a2, op0, op1, **kw), [a, s1, s2], wr)

    def stt(self, out, a, s, b, op0, op1):
        o, x, y, sc = _ap(out), _ap(a), _ap(b), _ap(s)
        self.op("dve", lambda e: e.scalar_tensor_tensor(o, x, sc, y, op0, op1), [a, s, b], [out])

    def copy(self, out, in_, eng="dve"):
        o, i = _ap(out), _ap(in_)
        if eng == "act":
            self.op("act", lambda e: e.copy(o, i), [in_], [out])
        else:
            self.op(eng, lambda e: e.tensor_copy(o, i), [in_], [out])

    def memset(self, out, val, eng="dve"):
        o = _ap(out)
        self.op(eng, lambda e: e.memset(o, val), [], [out])

    def select(self, out, mask, on_true, on_false):
        o, m, t, f = _ap(out), _ap(mask), _ap(on_true), _ap(on_false)
        self.op("dve", lambda e: e.tensor_copy(o, f), [on_false], [out])
        self.op("dve", lambda e: e.copy_predicated(o, m, t), [mask, on_true, out], [out])

    def recip(self, out, in_):
        o, i = _ap(out), _ap(in_)
        self.op("dve", lambda e: e.reciprocal(o, i), [in_], [out])

    def emit(self, final_waits=True):
        nc = self.nc
        tail = {e: [] for e in self.ENGS}
        if final_waits:
            for j, sem in enumerate(self.dsems):
                if self.dcnt[j] > 0:
                    tail["sp"].append((sem, self.dcnt[j]))
            for e in self.ENGS:
                if e != "sp" and self.ecnt[e] > 0:
                    tail["sp"].append((self.esem[e], self.ecnt[e]))
        with nc.Block() as blk:
            def run(ename):
                def body(eng):
                    for waits, fn, sem, inc in self.items[ename]:
                        for (s, v) in waits:
                            eng.wait_ge(s, v)
                        if fn is not None:
                            fn(eng).then_inc(sem, inc)
                    for (s, v) in tail[ename]:
                        eng.wait_ge(s, v)
                return body
            blk.tensor(run("pe"))
            blk.scalar(run("act"))
            blk.vector(run("dve"))
            blk.gpsimd(run("pool"))
            blk.sync(run("sp"))


class _Scope:
    def __init__(self, S):
        self.S = S

    def __enter__(self):
        import contextlib
        self.S.stacks.append(contextlib.ExitStack())
        return self

    def __exit__(self, *a):
        self.S.barrier()
        self.S.stacks.pop().close()
        return False


D_MODEL = 1024
DEPTH = 2
CTX = 256
N_IN = 6720
FF = 4096
EPS = 1e-6
NCORES = 8
SW = 1792
ND = 22
NPF = 106 + 4 * DEPTH
NPR = 576 + 512 * DEPTH + 2048
C_ID, C_UF, C_UB, C_VF, C_VB, C_BLK, C_OA, C_OB = 0, 128, 256, 384, 512, 640, 768, 896
C_U64F, C_U64B, C_NEGF, C_NEGB, C_ONES, C_EXP = 1024, 1088, 1152, 1216, 1280, 1408
NCST = 1408 + 1024


class Cfg:
    def __init__(self, seq):
        self.SEQ = seq
        self.LAT = seq // NCORES
        self.NT = CTX + self.LAT
        self.NU = self.NT // 256
        self.NCH = self.NT // 64


def make_consts():
    c = np.zeros((128, NCST), np.float32)
    j = np.arange(128)[:, None]
    t = np.arange(128)[None, :]
    same = (j // 64) == (t // 64)
    c[:, C_ID:C_ID + 128] = np.eye(128)
    c[:, C_UF:C_UF + 128] = same & (j <= t)
    c[:, C_UB:C_UB + 128] = same & (j >= t)
    c[:, C_VF:C_VF + 128] = same & (j > t)
    c[:, C_VB:C_VB + 128] = same & (j < t)
    c[:, C_BLK:C_BLK + 128] = same
    c[:, C_OA:C_OA + 128] = (j < 64) & (t >= 0)
    c[:, C_OB:C_OB + 128] = (j >= 64) & (t >= 0)
    t64 = np.arange(64)[None, :]
    jm = j % 64
    c[:, C_U64F:C_U64F + 64] = jm <= t64
    c[:, C_U64B:C_U64B + 64] = jm >= t64
    c[:, C_NEGF:C_NEGF + 64] = np.where(t64 >= jm, 0.0, -30000.0)
    c[:, C_NEGB:C_NEGB + 64] = np.where(t64 <= jm, 0.0, -30000.0)
    c[:, C_ONES:C_ONES + 128] = 1.0
    for h in range(16):
        kb, hh = h // 2, h % 2
        c[h, C_EXP + kb * 128 + hh * 64: C_EXP + kb * 128 + hh * 64 + 64] = 1.0
    return c


def wc(col):
    return col - 1024 if col < 3616 else col - 4128 + 2592


def wcg(col):
    if col < 1024:
        return col
    if col < 4128:
        return col - 3616 + 1024
    return col - 6208 + 1536


class Rot:
    def __init__(self, S, name, shape, dt, n):
        self.t = [S.sb(name + str(i), shape, dt) for i in range(n)]
        self.i = 0

    def next(self):
        v = self.t[self.i]
        self.i = (self.i + 1) % len(self.t)
        return v


def v3(v, a):
    return v.rr("p (a b) -> p a b", a=a)


def build(cfg, mode):
    nc = bass.Bass("TRN2", target_bir_lowering=False)
    S = Sched(nc)
    NT, NU, NCH, LAT = cfg.NT, cfg.NU, cfg.NCH, cfg.LAT
    fused = mode == "ALL"
    layers = [0, 1] if fused else [int(mode[-1])]
    doB = fused or mode[0] == "B"
    doC = fused or mode[0] == "C"

    in_names = []

    def din(name, shape, dt=F32):
        in_names.append(name)
        return S.dram(name, shape, dt, "ExternalInput")

    def dout(name, shape, dt=F32):
        return S.dram(name, shape, dt, "ExternalOutput")

    def dtmp(name, shape, dt=F32):
        return S.dram(name, shape, dt, "Internal")

    xin = din("xin", [NT, D_MODEL])
    cvec = din("cvec", [128, 8, 2])
    w_mod = din("w_mod", [DEPTH, D_MODEL, 6 * D_MODEL])
    w_in = din("w_in", [DEPTH, D_MODEL, N_IN])
    if doC:
        w_out = din("w_out", [DEPTH, 2048, D_MODEL])
        w_mlp1 = din("w_mlp1", [DEPTH, D_MODEL, FF])
        w_mlp2 = din("w_mlp2", [DEPTH, FF, D_MODEL])
    cst_d = din("cst", [128, NCST])
    pfm_d = din("pfm", [DEPTH, 128, NPF])
    prow_d = din("prow", [DEPTH, NPR])
    wgk_d = din("wgk", [DEPTH, 16, 2, 256])
    bmodfm_d = din("bmodfm", [DEPTH, 128, 48])
    fng_d = din("fng", [D_MODEL])
    mk_d = din("mk", [128, 16])

    def scratch(name, shape, dt, prod, cons):
        if fused:
            return dtmp(name, shape, dt)
        if doB:
            return dout(name, shape, dt)
        return din(name, shape, dt)

    Yssd = scratch("Yssd", [1024, NT], BF16, "B", "C")
    Yg = scratch("Yg", [512, NT], BF16, "B", "C")
    Yh = scratch("Yh", [512, NT], BF16, "B", "C")
    QG = scratch("QG", [2, 256, NT], BF16, "B", "C")
    QH = scratch("QH", [2, 512, NT], BF16, "B", "C")
    CTd = scratch("CTd", [256, NT], BF16, "B", "C")
    BTd = scratch("BTd", [16, 2, NT], F32, "B", "C")
    SlocB = scratch("SlocB", [NCH, 128, SW], BF16, "B", "C")
    Dall = scratch("Dall", [NCH, 128, 3, ND], F32, "B", "C")
    SsumD = scratch("SsumD", [4, 128, SW], F32, "B", "C")
    DsegD = scratch("DsegD", [2, 128, ND], F32, "B", "C")
    if fused:
        SsumAll = dtmp("SsumAll", [NCORES * 2 * 128, SW])
        DsegAll = dtmp("DsegAll", [NCORES * 2 * 128, ND])
        xcur = dtmp("xcur", [NT, D_MODEL])
    else:
        if doC:
            SsumAll = din("SsumAll", [NCORES * 2 * 128, SW])
            DsegAll = din("DsegAll", [NCORES * 2 * 128, ND])
        xcur = None
    if fused:
        outd = dout("out", [LAT, D_MODEL])
        xsrc = {0: xin, 1: xcur}
        xdst = {0: xcur, 1: outd}
    elif doC:
        l = layers[0]
        outd = dout("xout", [NT if l == 0 else LAT, D_MODEL])
        xsrc = {l: xin}
        xdst = {l: outd}
    else:
        xsrc = {layers[0]: xin}
        xdst = {}

    cst = S.sb("cst", [128, NCST], F32)
    S.dma("sp", cst, cst_d)
    ident = cst[:, C_ID:C_ID + 128]
    Ufb = [cst[:, C_UF:C_UF + 128], cst[:, C_UB:C_UB + 128]]
    Vfb = [cst[:, C_VF:C_VF + 128], cst[:, C_VB:C_VB + 128]]
    Blk = cst[:, C_BLK:C_BLK + 128]
    OnesAB = [cst[:, C_OA:C_OA + 128], cst[:, C_OB:C_OB + 128]]
    U64 = [cst[:, C_U64F:C_U64F + 64], cst[:, C_U64B:C_U64B + 64]]
    NEG = [cst[:, C_NEGF:C_NEGF + 64], cst[:, C_NEGB:C_NEGB + 64]]
    ONES = cst[:, C_ONES:C_ONES + 128]
    EXP16 = cst[:, C_EXP:C_EXP + 1024]
    mski = [S.sb("mski%d" % d, [128, 128], I32) for d in range(2)]
    for d in range(2):
        S.copy(mski[d], Ufb[d])
    zeros = S.sb("zeros", [128, 128], F32)
    S.memset(zeros, 0.0)
    mk = S.sb("mk", [128, 16], F32)
    S.dma("sp", mk, mk_d)

    PS = [S.ps("bank%d" % i, [128, 512], F32) for i in range(8)]

    A1 = S.sb("A1", [128, 8, 2], F32)
    SH1 = S.sb("SH1", [128, 8, 2], F32)
    A2 = S.sb("A2", [128, 8, 2], F32)
    SH2 = S.sb("SH2", [128, 8, 2], F32)
    GBC = S.sb("GBC", [128, 2, 2, 1024], F32)
    pfm = S.sb("pfm", [128, NPF], F32)
    modfm = S.sb("modfm", [128, 48, 2], F32)
    bmodfm = S.sb("bmodfm", [128, 48], F32)
    sc = S.sb("sc", [128, 8, 2], F32)
    cv = S.sb("cv", [128, 8, 2], F32)
    S.dma("sp", cv, cvec)
    S.act(sc, cv, AF.Silu)

    xrot = Rot(S, "xt", [128, D_MODEL], F32, 2)
    xnrot = Rot(S, "xn", [128, D_MODEL], F32, 2)
    small = Rot(S, "sm", [128, 4], F32, 4)

    def phaseA(l):
        scb = S.sb("scb", [128, 8, 2, 128], F32)
        S.copy(scb.rr("p a b c -> p (a b) c"), sc.rr("p a (b o) -> p (a b) o", o=1).bcast([128, 16, 128]))
        S.dma("sp", pfm, pfm_d[l])
        S.dma("sp", bmodfm, bmodfm_d[l])
        wmrot = Rot(S, "wm", [128, 8, 512], F32, 2)
        psA = PS[0]
        for ci in range(12):
            wm = wmrot.next()
            src = w_mod[l].rr("(kb p) n -> p kb n", p=128)
            S.dma("sp", wm[:, 0:4, :], src[:, 0:4, ci * 512:(ci + 1) * 512])
            S.dma("sp", wm[:, 4:8, :], src[:, 4:8, ci * 512:(ci + 1) * 512])
            for b in range(4):
                blk = ci * 4 + b
                for kb in range(8):
                    S.mm(psA[:, blk * 2:blk * 2 + 2], wm[:, kb, b * 128:(b + 1) * 128], sc[:, kb, :],
                         start=(kb == 0), stop=(kb == 7))
            if ci in (4, 5, 10, 11):
                which = 0 if ci < 6 else 1
                half = ci % 2
                boff = 576 + 512 * DEPTH + which * 1024 + half * 512
                for j in range(2):
                    pg = PS[1 + j]
                    for kb in range(8):
                        S.mm(pg[:, 0:512], scb[:, kb, j, :], wm[:, kb, :], start=(kb == 0), stop=(kb == 7))
                    dst = GBC[:, j, which, half * 512:(half + 1) * 512]
                    S.dma("sp", dst, V(prow_d.ap[l, boff:boff + 512].partition_broadcast(128), prow_d.res))
                    S.tt(dst, pg[:, 0:512], dst, ALU.add)
        S.tt(modfm, psA[:, 0:96].rr("p (a b) -> p a b", b=2), bmodfm.rr("p (a o) -> p a o", o=1).bcast([128, 48, 2]), ALU.add)
        ng1 = pfm[:, 0:8].rr("p (a o) -> p a o", o=1).bcast([128, 8, 2])
        ng2 = pfm[:, 8:16].rr("p (a o) -> p a o", o=1).bcast([128, 8, 2])
        S.stt(A1, modfm[:, 8:16, :], 1.0, ng1, ALU.add, ALU.mult)
        S.copy(SH1, modfm[:, 0:8, :])
        S.stt(A2, modfm[:, 32:40, :], 1.0, ng2, ALU.add, ALU.mult)
        S.copy(SH2, modfm[:, 24:32, :])

    def norm_T(xt, A, SH, j, hT, col0, pbank0):
        sm = small.next()
        xn = xnrot.next()
        S.act(xn, xt, AF.Square, accum=sm[:, 0:1])
        S.ts(sm[:, 1:2], sm[:, 0:1], 1.0 / D_MODEL, ALU.mult, EPS, ALU.add)
        S.act(sm[:, 2:3], sm[:, 1:2], AF.Sqrt)
        S.recip(sm[:, 3:4], sm[:, 2:3])
        S.ts(xn, xt, sm[:, 3:4], ALU.mult)
        for kb in range(8):
            pb = PS[pbank0 + kb // 4]
            S.tr(pb[:, (kb % 4) * 128:(kb % 4 + 1) * 128], xn[:, kb * 128:(kb + 1) * 128], ident)
            S.ts(hT[:, kb, col0:col0 + 128], pb[:, (kb % 4) * 128:(kb % 4 + 1) * 128],
                 A[:, kb, j:j + 1], ALU.mult, SH[:, kb, j:j + 1], ALU.add)

    def load_norm_T(xsrc_ap, tok0, ntile, A, SH, j, hT, pbank0):
        xts = []
        for i in range(ntile):
            xt = xrot.next()
            S.dma("sp", xt, xsrc_ap[tok0 + i * 128: tok0 + (i + 1) * 128, :])
            xts.append(xt)
            norm_T(xt, A, SH, j, hT, i * 128, pbank0)
        return xts

    wstg = Rot(S, "wstg", [128, 1024], F32, 2)

    def load_w_cast(dst, src_rows_ap, col0, ncols, dcol0, row0=0):
        nkb = dst.shape[1]
        for kb in range(nkb):
            c = 0
            while c < ncols:
                n = min(1024, ncols - c)
                stg = wstg.next()
                S.dma("sp", stg[:, 0:n], src_rows_ap[row0 + kb * 128:row0 + (kb + 1) * 128, col0 + c:col0 + c + n])
                S.copy(dst[:, kb, dcol0 + c:dcol0 + c + n], stg[:, 0:n], eng="pool")
                c += n

    def bc3(v, n, q):
        P = v.shape[0]
        return v.rr("p (h o) -> p h o", o=1).bcast([P, n, q])

    def seq_state(Wd, NDg):
        st = dict(
            S32=S.sb("S32", [128, Wd], F32), Sbf=S.sb("Sbf", [128, Wd], BF16), SsB=S.sb("SsB", [128, Wd], F32),
            Pf=S.sb("Pf", [128, NDg], F32), Pb=S.sb("Pb", [128, NDg], F32),
            stg=Rot(S, "stg", [128, Wd], BF16, 2), dstg=Rot(S, "dstg", [128, 3, NDg], F32, 2))
        return st

    def reset_states(st):
        S.memset(st["S32"], 0.0)
        S.memset(st["SsB"], 0.0)
        S.memset(st["Pf"], 1.0)
        S.memset(st["Pb"], 1.0)

    def seq_end(st, u, SO, Wd, DO, NDg):
        isctx = (u == 0)
        if isctx or u == NU - 1:
            base = 0 if isctx else 2
            S.dma("act", SsumD[base][:, SO:SO + Wd], st["S32"])
            S.dma("act", SsumD[base + 1][:, SO:SO + Wd], st["SsB"])
            if not isctx:
                S.dma("act", DsegD[0][:, DO:DO + NDg], st["Pf"])
                S.dma("act", DsegD[1][:, DO:DO + NDg], st["Pb"])
            if isctx:
                reset_states(st)

    def phaseB_ssd(l):
        xs_ap = xsrc[l]
        WB = S.sb("WBs", [128, 8, 1568], BF16)
        load_w_cast(WB, w_in[l], 1024, 1568, 0)
        dtb = S.sb("dtb", [128, 32], F32)
        abc = S.sb("abc", [128, 32], F32)
        S.dma("sp", dtb, V(prow_d.ap[l, 0:32].partition_broadcast(128), prow_d.res))
        S.dma("sp", abc, V(prow_d.ap[l, 32:64].partition_broadcast(128), prow_d.res))
        S.act(abc, abc, AF.Exp)
        S.ts(abc, abc, -1.0, ALU.mult)
        st = seq_state(1024, 16)
        S32, Sbf, SsB, Pf, Pb = st["S32"], st["Sbf"], st["SsB"], st["Pf"], st["Pb"]
        reset_states(st)
        tmpS = S.sb("tmpS", [128, 1024], F32)
        hTrot = Rot(S, "hT", [128, 8, 256], BF16, 2)
        accrot = Rot(S, "acc", [128, 256], F32, 3)
        xsF = S.sb("xsF", [128, 8, 256], F32)
        BCf = S.sb("BCf", [128, 4, 256], F32)
        BCb = S.sb("BCb", [128, 4, 256], BF16)
        Ysb = S.sb("Ysb", [128, 8, 256], BF16)
        dtr = S.sb("dtr", [128, 32], F32)
        dte = S.sb("dte", [128, 32], F32)
        dtt = S.sb("dtt", [128, 32], F32)
        gss = S.sb("gss", [128, 32], F32)
        btm = S.sb("btm", [128, 32], F32)
        eT = S.sb("eT", [128, 32], F32)
        w2 = S.sb("w2", [128, 32], F32)
        bTc = S.sb("bTc", [16, 2, 128], F32)
        Dc = S.sb("Dc", [128, 2, 32], F32)
        Ef = S.sb("Ef", [128, 8, 128], F32)
        xstm = S.sb("xstm", [128, 1024], F32)
        Btm = S.sb("Btm", [128, 2, 128], BF16)
        vv = [S.sb("vv%d" % d, [128, 1024], BF16) for d in range(2)]
        vv2 = [S.sb("vv2%d" % d, [128, 1024], BF16) for d in range(2)]
        Xp = S.sb("Xp", [128, 1024], F32)
        Dm = S.sb("Dm", [128, 1024], F32)
        LT = S.sb("LT", [128, 1024], F32)
        att = [S.sb("att%d" % d, [128, 16, 128], BF16) for d in range(2)]
        for d in range(2):
            S.memset(att[d], 0.0)
        GTc = S.sb("GTc", [128, 2, 64], F32)
        tmpY = S.sb("tmpY", [128, 8, 128], F32)

        ch = 0
        for u in range(NU):
            isctx = (u == 0)
            j = 1 if isctx else 0
            tok0 = u * 256
            hT = hTrot.next()
            load_norm_T(xs_ap, tok0, 2, A1, SH1, j, hT, 0)
            R, Lr = (1, 256) if isctx else (4, 64)
            for blk in range(12):
                ps = PS[2 + blk % 2][:, 0:256]
                for kb in range(8):
                    S.mm(ps, WB[:, kb, blk * 128:(blk + 1) * 128], hT[:, kb, :], start=(kb == 0), stop=(kb == 7))
                acc = accrot.next()
                S.act(acc, ps, AF.Identity, bias=pfm[:, 84 + blk:85 + blk], scale=pfm[:, 24 + blk * 5 + 2:24 + blk * 5 + 3])
                accv = acc.rr("p (r l) -> p r l", r=R)
                psv = ps.rr("p (r l) -> p r l", r=R)
                for jt, s_ in ((0, -2), (1, -1), (3, 1), (4, 2)):
                    if s_ < 0:
                        dstv, srcv = accv[:, :, -s_:Lr], psv[:, :, 0:Lr + s_]
                    else:
                        dstv, srcv = accv[:, :, 0:Lr - s_], psv[:, :, s_:Lr]
                    S.stt(dstv, srcv, pfm[:, 24 + blk * 5 + jt:24 + blk * 5 + jt + 1], dstv, ALU.mult, ALU.add)
                if blk < 8:
                    S.act(xsF[:, blk, :], acc, AF.Silu)
                else:
                    S.act(BCf[:, blk - 8, :], acc, AF.Silu)
            S.copy(BCb, BCf)
            S.dma("act", CTd.rr("(g p) t -> p g t", p=128)[:, :, tok0:tok0 + 256], BCb[:, 2:4, :])
            for i in range(2):
                pc = i * 128
                ptok = tok0 + pc
                for kb in range(8):
                    S.mm(PS[4][:, 0:32], hT[:, kb, pc:pc + 128], WB[:, kb, 1536:1568], start=(kb == 0), stop=(kb == 7))
                S.tt(dtr, PS[4][:, 0:32], dtb, ALU.add)
                S.act(dte, dtr, AF.Exp)
                S.act(dtt, dte, AF.Ln, bias=1.0)
                S.tt(gss, dtt, abc, ALU.mult)
                p5 = PS[5]
                for d in range(2):
                    gd = gss[:, d * 16:(d + 1) * 16]
                    S.mm(p5[:, d * 16:(d + 1) * 16], Ufb[d], gd)
                    S.mm(p5[:, 32 + d * 16:32 + (d + 1) * 16], Vfb[d], gd)
                    S.mm(p5[0:16, 64 + d * 128:64 + (d + 1) * 128], gd, Ufb[d])
                for c in range(2):
                    S.mm(p5[:, 320 + c * 32:320 + (c + 1) * 32], OnesAB[c], gss)
                S.copy(btm, p5[:, 0:32])
                S.act(eT, p5[:, 32:64], AF.Exp)
                S.copy(bTc.rr("p a b -> p (a b)"), p5[0:16, 64:320])
                S.act(Dc.rr("p a b -> p (a b)"), p5[:, 320:384], AF.Exp)
                S.dma("act", BTd[:, :, ptok:ptok + 128], bTc)
                S.tt(w2, dtt, eT, ALU.mult)
                for kb in range(8):
                    S.mm(PS[6 + kb // 4][:, (kb % 4) * 128:(kb % 4 + 1) * 128], EXP16[0:16, kb * 128:(kb + 1) * 128], bTc[0:16, 0, :])
                for hf_ in range(2):
                    S.act(Ef[:, hf_ * 4:(hf_ + 1) * 4, :].rr("p a b -> p (a b)"), PS[6 + hf_], AF.Exp)
                for kb in range(8):
                    S.tr(PS[6 + kb // 4][:, (kb % 4) * 128:(kb % 4 + 1) * 128], xsF[:, kb, pc:pc + 128], ident)
                for hf_ in range(2):
                    S.copy(xstm[:, hf_ * 512:(hf_ + 1) * 512], PS[6 + hf_], eng="act")
                for g in range(2):
                    S.tr(PS[4][:, g * 128:(g + 1) * 128], BCf[:, g, pc:pc + 128], ident)
                S.copy(Btm.rr("p a b -> p (a b)"), PS[4][:, 0:256])
                xv = xstm.rr("p (h q) -> p h q", q=64)
                for d in range(2):
                    S.tt(vv[d].rr("p (h q) -> p h q", q=64), xv, bc3(dtt[:, d * 16:(d + 1) * 16], 16, 64), ALU.mult)
                    S.tt(vv2[d].rr("p (h q) -> p h q", q=64), xv, bc3(w2[:, d * 16:(d + 1) * 16], 16, 64), ALU.mult)
                for g in range(2):
                    S.mm(PS[4][:, g * 128:(g + 1) * 128], BCb[:, g, pc:pc + 128], BCb[:, 2 + g, pc:pc + 128])
                for g in range(2):
                    S.copy(GTc[0:64, g, :], PS[4][0:64, g * 128:g * 128 + 64])
                    S.copy(GTc[64:128, g, :], PS[4][64:128, g * 128 + 64:g * 128 + 128])
                for d in range(2):
                    S.tt(Xp.rr("p (h t) -> p h t", t=64), bc3(gss[:, d * 16:(d + 1) * 16], 16, 64),
                         U64[d].rr("p (o t) -> p o t", o=1).bcast([128, 16, 64]), ALU.mult)
                    for hf_ in range(2):
                        S.mm(PS[6 + hf_], Blk, Xp[:, hf_ * 512:(hf_ + 1) * 512])
                        S.tt(Dm[:, hf_ * 512:(hf_ + 1) * 512].rr("p (h t) -> p h t", t=64),
                             PS[6 + hf_].rr("p (h t) -> p h t", t=64),
                             bc3(btm[:, d * 16 + hf_ * 8:d * 16 + hf_ * 8 + 8], 8, 64), ALU.subtract)
                    S.tt(Dm.rr("p (h t) -> p h t", t=64), Dm.rr("p (h t) -> p h t", t=64),
                         NEG[d].rr("p (o t) -> p o t", o=1).bcast([128, 16, 64]), ALU.add)
                    S.act(LT, Dm, AF.Exp)
                    for c in range(2):
                        rc = slice(c * 64, (c + 1) * 64)
                        S.tt(att[d][rc].rr("p (g h) t -> p g h t", g=2)[:, :, :, c * 64:(c + 1) * 64],
                             LT[rc].rr("p (g h t) -> p g h t", g=2, t=64),
                             GTc[rc].rr("p g (o t) -> p g o t", o=1).bcast([64, 2, 8, 64]), ALU.mult)
                for kb in range(8):
                    for hh in range(2):
                        h = 2 * kb + hh
                        for d in range(2):
                            S.mm(PS[kb // 4][hh * 64:(hh + 1) * 64, (kb % 4) * 128:(kb % 4 + 1) * 128],
                                 vv[d][:, h * 64:(h + 1) * 64], att[d][:, h, :], start=(d == 0), stop=(d == 1))
                for c in range(2):
                    rc = slice(c * 64, (c + 1) * 64)
                    stt_ = st["stg"].next()
                    dst = st["dstg"].next()
                    S.copy(dst[:, 2, :], Pf)
                    S.copy(Sbf, S32)
                    for kb in range(8):
                        S.mm(PS[6 + kb // 4][:, (kb % 4) * 128 + c * 64:(kb % 4) * 128 + (c + 1) * 64],
                             Sbf[:, kb * 128:(kb + 1) * 128], BCb[:, 2 + kb // 4, pc + c * 64:pc + (c + 1) * 64])
                    for hf_ in range(2):
                        pv = PS[6 + hf_].rr("p (k t) -> p k t", t=128)[:, :, c * 64:(c + 1) * 64]
                        S.tt(tmpY[:, hf_ * 4:(hf_ + 1) * 4, c * 64:(c + 1) * 64], pv, Ef[:, hf_ * 4:(hf_ + 1) * 4, c * 64:(c + 1) * 64], ALU.mult)
                    for d in range(2):
                        for g in range(2):
                            S.mm(PS[2 + g], Btm[rc, g, :], vv2[d][rc, g * 512:(g + 1) * 512])
                        for g in range(2):
                            sl = slice(g * 512, (g + 1) * 512)
                            if d == 0:
                                S.tt(S32[:, sl].rr("p (h q) -> p h q", q=64), S32[:, sl].rr("p (h q) -> p h q", q=64),
                                     bc3(Dc[:, c, g * 8:(g + 1) * 8], 8, 64), ALU.mult)
                                S.tt(S32[:, sl], S32[:, sl], PS[2 + g], ALU.add)
                            else:
                                S.copy(stt_[:, sl], PS[2 + g], eng="act")
                                S.tt(tmpS[:, sl].rr("p (h q) -> p h q", q=64), PS[2 + g].rr("p (h q) -> p h q", q=64),
                                     bc3(Pb[:, g * 8:(g + 1) * 8], 8, 64), ALU.mult)
                                S.tt(SsB[:, sl], SsB[:, sl], tmpS[:, sl], ALU.add)
                        S.copy(dst[:, d, :], Dc[:, c, d * 16:(d + 1) * 16])
                    S.dma("act", SlocB[ch][:, 0:1024], stt_)
                    S.dma("act", Dall[ch][:, :, 0:16], dst)
                    S.tt(Pf, Pf, dst[:, 0, :], ALU.mult)
                    S.tt(Pb, Pb, dst[:, 1, :], ALU.mult)
                    ch += 1
                for hf_ in range(2):
                    S.tt(tmpY[:, hf_ * 4:(hf_ + 1) * 4, :].rr("p a b -> p (a b)"), tmpY[:, hf_ * 4:(hf_ + 1) * 4, :].rr("p a b -> p (a b)"),
                         PS[hf_], ALU.add)
                for kb in range(8):
                    S.stt(Ysb[:, kb, pc:pc + 128], xsF[:, kb, pc:pc + 128], pfm[:, 96 + kb:97 + kb], tmpY[:, kb, :], ALU.mult, ALU.add)
            S.dma("act", Yssd.rr("(kb p) t -> p kb t", p=128)[:, :, tok0:tok0 + 256], Ysb)
            seq_end(st, u, 0, 1024, 0, 16)

    def phaseB_g(l, kind):
        xs_ap = xsrc[l]
        if kind == "gla":
            nb, hpb, dk, SO, Wd, DO, NDg, gam = 2, 2, 64, 1024, 256, 16, 2, -1.0 / 16.0
            WB = S.sb("WBg", [128, 8, 1056], BF16)
            load_w_cast(WB, w_in[l], 2592, 1024, 0)
            load_w_cast(WB, w_in[l], 4128, 32, 1024)
            bgk = S.sb("bgk", [128, 512], F32)
            S.dma("sp", bgk, V(prow_d.ap[l, 64:576].partition_broadcast(128), prow_d.res))
            wgk = S.sb("wgk", [16, 2, 256], F32)
            S.dma("sp", wgk, wgk_d[l])
            Yd, Qd = Yg, QG
        else:
            nb, hpb, dk, SO, Wd, DO, NDg, gam = 4, 1, 128, 1280, 512, 18, 4, 1.0
            WB = S.sb("WBh", [128, 8, 2048], BF16)
            load_w_cast(WB, w_in[l], 4160, 2048, 0)
            Yd, Qd = Yh, QH
            lgt = S.sb("lgt", [128, DEPTH, 512], F32)
            for ll in range(DEPTH):
                S.dma("sp", lgt[:, ll, :], V(prow_d.ap[l, 576 + ll * 512:576 + (ll + 1) * 512].partition_broadcast(128), prow_d.res))
            lb_bc = S.sb("lb_bc", [128, 512], F32)
            oml_bc = S.sb("oml_bc", [128, 512], F32)
            lb_fm = S.sb("lb_fm", [128, 4], F32)
            oml_fm = S.sb("oml_fm", [128, 4], F32)
            S.act(lgt, lgt, AF.Exp)
            den = S.sb("den", [128, 512], F32)
            S.copy(den, lgt[:, 0, :])
            for ll in range(1, DEPTH):
                S.tt(den, den, lgt[:, ll, :], ALU.add)
            S.recip(den, den)
            S.memset(lb_bc, 0.0)
            for ll in range(1, l + 1):
                S.tt(lb_bc, lb_bc, lgt[:, ll, :], ALU.add)
            S.tt(lb_bc, lb_bc, den, ALU.mult)
            S.ts(oml_bc, lb_bc, -1.0, ALU.mult, 1.0, ALU.add)
            efm = S.sb("efm", [128, DEPTH, 4], F32)
            S.act(efm, pfm[:, 106:106 + 4 * DEPTH].rr("p (a b) -> p a b", b=4), AF.Exp)
            denf = S.sb("denf", [128, 4], F32)
            S.copy(denf, efm[:, 0, :])
            for ll in range(1, DEPTH):
                S.tt(denf, denf, efm[:, ll, :], ALU.add)
            S.recip(denf, denf)
            S.memset(lb_fm, 0.0)
            for ll in range(1, l + 1):
                S.tt(lb_fm, lb_fm, efm[:, ll, :], ALU.add)
            S.tt(lb_fm, lb_fm, denf, ALU.mult)
            S.ts(oml_fm, lb_fm, -1.0, ALU.mult, 1.0, ALU.add)
            sgt = S.sb("sgt", [128, 512], F32)
            ut = S.sb("ut", [128, 512], F32)
        nh = nb * hpb
        st = seq_state(Wd, NDg)
        S32, Sbf, SsB, Pf, Pb = st["S32"], st["Sbf"], st["SsB"], st["Pf"], st["Pb"]
        reset_states(st)
        hTrot = Rot(S, "hT", [128, 8, 256], BF16, 2)
        accrot = Rot(S, "acc", [128, 256], F32, 2)
        Ys = S.sb("Ys", [128, 4, 256], BF16)
        qf = S.sb("qf", [128, nb, 256], F32)
        kf = [S.sb("kf%d" % d, [128, nb, 256], F32) for d in range(2 if kind == "hg" else 1)]
        if kind == "gla":
            kf = [kf[0], kf[0]]
            glrT = [S.sb("glrT%d" % d, [16, 256], F32) for d in range(2)]
            xg = S.sb("xg", [128, 256], F32)
        sp = [S.sb("sp%d" % d, [128, nb * 128], F32) for d in range(2)]
        ktm = [S.sb("ktm%d" % d, [128, nb * 128], F32) for d in range(2 if kind == "hg" else 1)]
        if kind == "gla":
            ktm = [ktm[0], ktm[0]]
        vtm = S.sb("vtm", [128, 512], BF16)
        erot = Rot(S, "E", [128, 128], F32, 3)
        colr = Rot(S, "colr", [128, 4], F32, 3)
        arot = Rot(S, "A", [128, 128], F32, 4)
        ekr = Rot(S, "Ek", [128, nb * 128], F32, 2)
        qt = [S.sb("qt%d" % d, [128, nb, hpb, 128], BF16) for d in range(2)]
        kt = [S.sb("kt%d" % d, [128, nb, 128], BF16) for d in range(2)]
        qh = [S.sb("qh%d" % d, [128, nb, hpb, 256], BF16) for d in range(2)]
        kh = [S.sb("kh%d" % d, [128, nb * 128], BF16) for d in range(2)]
        Dg = [S.sb("Dg%d" % d, [128, 2, nb], F32) for d in range(2)]
        As = [S.sb("As%d" % i, [128, 128], BF16) for i in range(nh)]
        if hpb > 1:
            for d in range(2):
                S.memset(qt[d], 0.0)
                S.memset(qh[d], 0.0)

        def fm_proj(c0, m, ps_out, hT):
            for kb in range(8):
                S.mm(ps_out, WB[:, kb, c0:c0 + m], hT[:, kb, :], start=(kb == 0), stop=(kb == 7))

        def tm_proj(c0, n, ps_out, hT, pc):
            for kb in range(8):
                S.mm(ps_out, hT[:, kb, pc:pc + 128], WB[:, kb, c0:c0 + n], start=(kb == 0), stop=(kb == 7))

        ch = 0
        for u in range(NU):
            isctx = (u == 0)
            j = 1 if isctx else 0
            tok0 = u * 256
            hT = hTrot.next()
            load_norm_T(xs_ap, tok0, 2, A1, SH1, j, hT, 0)
            if kind == "gla":
                for b in range(2):
                    ps = PS[2][:, 0:256]
                    fm_proj(b * 128, 128, ps, hT)
                    S.ts(qf[:, b, :], ps, 0.125, ALU.mult)
                    ps = PS[3][:, 0:256]
                    fm_proj(256 + b * 128, 128, ps, hT)
                    S.copy(kf[0][:, b, :], ps, eng="act")
                for d in range(2):
                    ps = PS[2 + d][0:16, 0:256]
                    fm_proj(1024 + d * 16, 16, ps, hT)
                    S.copy(glrT[d], ps)
            else:
                for b in range(4):
                    ps = PS[2 + b % 2][:, 0:256]
                    fm_proj(b * 128, 128, ps, hT)
                    S.act(qf[:, b, :], ps, AF.Silu)
                for d in range(2):
                    for b in range(4):
                        ps = PS[2 + b % 2][:, 0:256]
                        fm_proj(512 + (d * 4 + b) * 128, 128, ps, hT)
                        ac = accrot.next()
                        S.act(ac, ps, AF.Sigmoid, scale=-1.0)
                        S.ts(kf[d][:, b, :], ac, oml_fm[:, b:b + 1], ALU.mult)
            for i in range(2):
                pc = i * 128
                ptok = tok0 + pc
                if kind == "gla":
                    tm_proj(256, 256, PS[4][:, 0:256], hT, pc)
                    S.copy(ktm[0], PS[4][:, 0:256])
                    tm_proj(512, 512, PS[5], hT, pc)
                    S.copy(vtm, PS[5], eng="act")
                    for d in range(2):
                        S.mm(PS[4][:, 0:256], glrT[d][0:16, pc:pc + 128], wgk[0:16, d, :])
                        S.tt(xg, PS[4][:, 0:256], bgk[:, d * 256:(d + 1) * 256], ALU.add)
                        S.act(xg, xg, AF.Exp, scale=-1.0)
                        S.act(sp[d], xg, AF.Ln, bias=1.0)
                else:
                    for d in range(2):
                        tm_proj(512 + d * 512, 512, PS[5], hT, pc)
                        S.act(sgt, PS[5], AF.Sigmoid)
                        S.tt(ut, sgt, oml_bc, ALU.mult)
                        S.tt(ut, ut, lb_bc, ALU.add)
                        S.act(sp[d], ut, AF.Ln)
                        S.ts(sgt, sgt, -1.0, ALU.mult, 1.0, ALU.add)
                        S.tt(ktm[d], sgt, oml_bc, ALU.mult)
                    tm_proj(1536, 512, PS[5], hT, pc)
                    S.copy(vtm, PS[5], eng="act")
                for d in range(2):
                    for b in range(nb):
                        pa = PS[4][:, 0:128]
                        S.mm(pa, sp[d][:, b * 128:(b + 1) * 128], Ufb[d])
                        cr = colr.next()
                        for c in range(2):
                            ref = c * 64 + 32
                            S.ts(cr[:, c:c + 1], pa[:, ref:ref + 1], -gam, ALU.mult)
                            S.ts(cr[:, 2 + c:3 + c], pa[:, ref:ref + 1], gam, ALU.mult)
                        for c in range(2):
                            cols = slice(c * 64, (c + 1) * 64)
                            E1 = erot.next()
                            S.act(E1[:, 0:64], pa[:, cols], AF.Exp, bias=cr[:, c:c + 1], scale=gam)
                            S.act(E1[:, 64:128], pa[:, cols], AF.Exp, bias=cr[:, 2 + c:3 + c], scale=-gam)
                            for hh in range(hpb):
                                rw = slice(hh * dk, (hh + 1) * dk)
                                S.tt(qt[d][rw, b, hh, cols], qf[rw, b, pc + c * 64:pc + (c + 1) * 64], E1[rw, 0:64], ALU.mult)
                            S.tt(kt[d][:, b, cols], kf[d][:, b, pc + c * 64:pc + (c + 1) * 64], E1[:, 64:128], ALU.mult)
                            dcol = c * 64 + (63 if d == 0 else 0)
                            S.act(Dg[d][:, c, b:b + 1], pa[:, dcol:dcol + 1], AF.Exp, scale=gam)
                        E3 = erot.next()
                        S.act(E3, pa, AF.Exp, scale=gam)
                        for hh in range(hpb):
                            rw = slice(hh * dk, (hh + 1) * dk)
                            S.tt(qh[d][rw, b, hh, pc:pc + 128], qf[rw, b, pc:pc + 128], E3[rw, :], ALU.mult)
                    px = PS[5][:, 0:nb * 128]
                    S.mm(px, Vfb[d], sp[d])
                    Ek = ekr.next()
                    S.act(Ek, px, AF.Exp, scale=gam)
                    S.tt(kh[d], ktm[d], Ek, ALU.mult)
                for b in range(nb):
                    for hh in range(hpb):
                        Ad = []
                        for d in range(2):
                            pat = PS[4][:, 128 * (1 + d):128 * (2 + d)]
                            S.mm(pat, kt[d][:, b, :], qt[d][:, b, hh, :])
                            A = arot.next()
                            S.select(A, mski[d], pat, zeros)
                            Ad.append(A)
                        S.tt(As[b * hpb + hh], Ad[0], Ad[1], ALU.add)
                for d in range(2):
                    if hpb == 2:
                        for b in range(nb):
                            for hh in range(2):
                                S.dma("act", Qd[d][b * 128 + hh * 64:b * 128 + (hh + 1) * 64, ptok:ptok + 128],
                                      qh[d][hh * 64:(hh + 1) * 64, b, hh, pc:pc + 128])
                    else:
                        S.dma("act", Qd[d].rr("(b p) t -> p b t", p=128)[:, :, ptok:ptok + 128], qh[d][:, :, 0, pc:pc + 128])
                for c in range(2):
                    rows_c = slice(c * 64, (c + 1) * 64)
                    stt_ = st["stg"].next()
                    dst = st["dstg"].next()
                    S.copy(dst[:, 2, :], Pf)
                    S.copy(Sbf, S32)
                    for b in range(nb):
                        for hh in range(hpb):
                            hd = b * hpb + hh
                            po = PS[6][:, hd * 128 + c * 64:hd * 128 + (c + 1) * 64]
                            S.mm(po, vtm[:, hd * 128:(hd + 1) * 128], As[hd][:, c * 64:(c + 1) * 64], start=True, stop=False)
                            S.mm(po, Sbf[:, b * 128:(b + 1) * 128], qh[0][:, b, hh, pc + c * 64:pc + (c + 1) * 64], start=False, stop=True)
                            S.copy(Ys[:, hd, pc + c * 64:pc + (c + 1) * 64], po, eng="act")
                    for d in range(2):
                        psb = PS[7]
                        for b in range(nb):
                            for hh in range(hpb):
                                hd = b * hpb + hh
                                S.mm(psb[hh * dk:(hh + 1) * dk, b * 128:(b + 1) * 128],
                                     kh[d][rows_c, hd * dk:(hd + 1) * dk], vtm[rows_c, hd * 128:(hd + 1) * 128])
                        for b in range(nb):
                            sl = slice(b * 128, (b + 1) * 128)
                            if d == 0:
                                S.stt(S32[:, sl], S32[:, sl], Dg[0][:, c, b:b + 1], psb[:, sl], ALU.mult, ALU.add)
                            else:
                                S.copy(stt_[:, sl], psb[:, sl], eng="act")
                                S.stt(SsB[:, sl], psb[:, sl], Pb[:, b:b + 1], SsB[:, sl], ALU.mult, ALU.add)
                        S.copy(dst[:, d, :], Dg[d][:, c, :])
                    S.dma("act", SlocB[ch][:, SO:SO + Wd], stt_)
                    S.dma("act", Dall[ch][:, :, DO:DO + NDg], dst)
                    S.tt(Pf, Pf, dst[:, 0, :], ALU.mult)
                    S.tt(Pb, Pb, dst[:, 1, :], ALU.mult)
                    ch += 1
            S.dma("act", Yd.rr("(kb p) t -> p kb t", p=128)[:, :, tok0:tok0 + 256], Ys)
            seq_end(st, u, SO, Wd, DO, NDg)

    NP = NT // 128
    if doC:
        xmid = dtmp("xmid", [NT, D_MODEL])
        xacc = dtmp("xacc", [NT, D_MODEL])
        yTd = dout("yTd", [2048, NT], BF16) if (DEBUG_YT and not fused) else dtmp("yTd", [2048, NT], BF16)

    def apply_decay(St, Dt):
        S.tt(St[:, 0:1024].rr("p (h q) -> p h q", q=64), St[:, 0:1024].rr("p (h q) -> p h q", q=64), bc3(Dt[:, 0:16], 16, 64), ALU.mult)
        S.tt(St[:, 1024:1280].rr("p (h q) -> p h q", q=128), St[:, 1024:1280].rr("p (h q) -> p h q", q=128), bc3(Dt[:, 16:18], 2, 128), ALU.mult)
        S.tt(St[:, 1280:1792].rr("p (h q) -> p h q", q=128), St[:, 1280:1792].rr("p (h q) -> p h q", q=128), bc3(Dt[:, 18:22], 4, 128), ALU.mult)

    def phaseC(l):
        xs_ap = xsrc[l]
        last = (l == DEPTH - 1)
        WG = S.sb("WG", [128, 8, 2048], BF16)
        load_w_cast(WG, w_in[l], 0, 1024, 0)
        load_w_cast(WG, w_in[l], 3616, 512, 1024)
        load_w_cast(WG, w_in[l], 6208, 512, 1536)
        Sinf = S.sb("Sinf", [128, SW], F32)
        Sb = S.sb("Sb", [128, SW], F32)
        tmpC = S.sb("tmpC", [128, SW], F32)
        omm = S.sb("omm", [128, 16], F32)
        S.ts(omm, mk, -1.0, ALU.mult, 1.0, ALU.add)
        sj = S.sb("sj", [128, SW], F32)
        djr = Rot(S, "dj", [128, ND], F32, 2)
        S.dma("sp", Sinf, SsumD[0])
        S.dma("sp", Sb, SsumD[1])
        for dirn, order in ((0, range(NCORES)), (1, range(NCORES - 1, -1, -1))):
            St = Sinf if dirn == 0 else Sb
            for jj in order:
                dj = djr.next()
                S.dma("sp", sj, SsumAll[(jj * 2 + dirn) * 128:(jj * 2 + dirn + 1) * 128, :])
                S.dma("sp", dj, DsegAll[(jj * 2 + dirn) * 128:(jj * 2 + dirn + 1) * 128, :])
                mcol = mk[:, dirn * 8 + jj:dirn * 8 + jj + 1]
                S.stt(dj, dj, mcol, omm[:, dirn * 8 + jj:dirn * 8 + jj + 1].bcast([128, ND]), ALU.mult, ALU.add)
                apply_decay(St, dj)
                S.stt(St, sj, mcol, St, ALU.mult, ALU.add)
        Sbb = S.sb("Sbb", [128, 2, SW], BF16)
        Scb = S.sb("Scb", [128, 2, SW], BF16)
        slr = Rot(S, "slr", [128, SW], BF16, 2)
        dar = Rot(S, "dar", [128, 3, ND], F32, 2)
        hTrot = Rot(S, "hTc", [128, 8, 128], BF16, 2)
        gz = S.sb("gz", [128, 16, 128], F32)
        Yl = S.sb("Yl", [128, 16, 128], BF16)
        CTl = S.sb("CTl", [128, 2, 128], BF16)
        btl = S.sb("btl", [16, 2, 128], F32)
        qgl = [S.sb("qgl%d" % d, [128, 2, 2, 128], BF16) for d in range(2)]
        qhl = [S.sb("qhl%d" % d, [128, 4, 128], BF16) for d in range(2)]
        for d in range(2):
            S.memset(qgl[d], 0.0)
        yT = Rot(S, "yT", [128, 16, 128], BF16, 2)
        Ed = S.sb("Ed", [128, 8, 128], F32)
        Yt = S.sb("Yt", [128, 8, 128], F32)
        t1 = S.sb("t1", [128, 8, 128], F32)
        r1 = S.sb("r1", [128, 128], F32)
        yh = Rot(S, "yh", [128, 128], F32, 2)
        sqh = Rot(S, "sqh", [128, 128], F32, 2)
        rh = Rot(S, "rh", [128, 128], F32, 2)

        seqs = [("lat", list(range(NP - 1, 1, -1)))]
        if not last:
            seqs.append(("ctx", [1, 0]))
        for name, pairs in seqs:
            has_corr = (name == "lat")
            if not has_corr:
                S.memset(Sb, 0.0)
            for p_ in pairs:
                tok0 = p_ * 128
                j = 1 if p_ < 2 else 0
                for cc in (1, 0):
                    chg = p_ * 2 + cc
                    sl_ = slr.next()
                    da = dar.next()
                    S.dma("sp", sl_, SlocB[chg])
                    S.dma("sp", da, Dall[chg])
                    S.copy(Sbb[:, cc, :], Sb)
                    if has_corr:
                        S.copy(tmpC, Sinf)
                        apply_decay(tmpC, da[:, 2, :])
                        S.copy(Scb[:, cc, :], tmpC)
                    apply_decay(Sb, da[:, 1, :])
                    S.tt(Sb, Sb, sl_, ALU.add)
                hT = hTrot.next()
                load_norm_T(xs_ap, tok0, 1, A1, SH1, j, hT, 0)
                for blk in range(16):
                    ps = PS[2 + blk % 2][:, 0:128]
                    for kb in range(8):
                        S.mm(ps, WG[:, kb, blk * 128:(blk + 1) * 128], hT[:, kb, :], start=(kb == 0), stop=(kb == 7))
                    S.act(gz[:, blk, :], ps, AF.Silu if blk < 12 else AF.Sigmoid)
                S.dma("sp", Yl[:, 0:8, :], Yssd.rr("(kb p) t -> p kb t", p=128)[:, :, tok0:tok0 + 128])
                S.dma("sp", Yl[:, 8:12, :], Yg.rr("(kb p) t -> p kb t", p=128)[:, :, tok0:tok0 + 128])
                S.dma("sp", Yl[:, 12:16, :], Yh.rr("(kb p) t -> p kb t", p=128)[:, :, tok0:tok0 + 128])
                S.dma("sp", CTl, CTd.rr("(g p) t -> p g t", p=128)[:, :, tok0:tok0 + 128])
                S.dma("sp", btl, BTd[:, :, tok0:tok0 + 128])
                for d in range(2):
                    for b in range(2):
                        for hh in range(2):
                            S.dma("sp", qgl[d][hh * 64:(hh + 1) * 64, b, hh, :], QG[d][b * 128 + hh * 64:b * 128 + (hh + 1) * 64, tok0:tok0 + 128])
                    S.dma("sp", qhl[d], QH[d].rr("(b p) t -> p b t", p=128)[:, :, tok0:tok0 + 128])
                yTv = yT.next()
                S.copy(Yt, Yl[:, 0:8, :])
                for (d, Sst) in ((1, Sbb), (0, Scb)):
                    if d == 0 and not has_corr:
                        continue
                    for kb in range(8):
                        S.mm(PS[4 + kb // 4][:, (kb % 4) * 128:(kb % 4 + 1) * 128], EXP16[0:16, kb * 128:(kb + 1) * 128], btl[0:16, d, :])
                    for hf_ in range(2):
                        S.act(Ed[:, hf_ * 4:(hf_ + 1) * 4, :].rr("p a b -> p (a b)"), PS[4 + hf_], AF.Exp)
                    for c in range(2):
                        for kb in range(8):
                            S.mm(PS[6 + kb // 4][:, (kb % 4) * 128 + c * 64:(kb % 4) * 128 + (c + 1) * 64],
                                 Sst[:, c, kb * 128:(kb + 1) * 128], CTl[:, kb // 4, c * 64:(c + 1) * 64])
                    for hf_ in range(2):
                        S.tt(t1[:, hf_ * 4:(hf_ + 1) * 4, :].rr("p a b -> p (a b)"), PS[6 + hf_], Ed[:, hf_ * 4:(hf_ + 1) * 4, :].rr("p a b -> p (a b)"), ALU.mult)
                    S.tt(Yt, Yt, t1, ALU.add)
                S.tt(Yt, Yt, gz[:, 0:8, :], ALU.mult)
                S.act(t1, Yt, AF.Square)
                for kb in range(8):
                    S.mm(PS[4][:, 0:128], ONES, t1[:, kb, :], start=(kb == 0), stop=(kb == 7))
                S.ts(r1, PS[4][:, 0:128], 1.0 / 1024.0, ALU.mult, EPS, ALU.add)
                S.act(r1, r1, AF.Sqrt)
                S.recip(r1, r1)
                for kb in range(8):
                    S.stt(yTv[:, kb, :], Yt[:, kb, :], pfm[:, 16 + kb:17 + kb], r1, ALU.mult, ALU.mult)
                for gi, (hpbg, SOg, ql) in enumerate(((2, 1024, qgl), (1, 1280, qhl))):
                    for hd in range(4):
                        b, hh = hd // hpbg, hd % hpbg
                        for c in range(2):
                            po = PS[5][:, hd * 128 + c * 64:hd * 128 + (c + 1) * 64]
                            if gi == 0:
                                qb_ = ql[1][:, b, hh, c * 64:(c + 1) * 64]
                                qf_ = ql[0][:, b, hh, c * 64:(c + 1) * 64]
                            else:
                                qb_ = ql[1][:, b, c * 64:(c + 1) * 64]
                                qf_ = ql[0][:, b, c * 64:(c + 1) * 64]
                            S.mm(po, Sbb[:, c, SOg + b * 128:SOg + (b + 1) * 128], qb_, start=True, stop=not has_corr)
                            if has_corr:
                                S.mm(po, Scb[:, c, SOg + b * 128:SOg + (b + 1) * 128], qf_, start=False, stop=True)
                        yv = yh.next()
                        S.tt(yv, PS[5][:, hd * 128:(hd + 1) * 128], Yl[:, 8 + gi * 4 + hd, :], ALU.add)
                        sv = sqh.next()
                        S.act(sv, yv, AF.Square)
                        S.mm(PS[4][:, 128:256], ONES, sv)
                        rv = rh.next()
                        S.ts(rv, PS[4][:, 128:256], 1.0 / 128.0, ALU.mult, EPS, ALU.add)
                        S.act(rv, rv, AF.Sqrt)
                        S.recip(rv, rv)
                        S.stt(yv, yv, pfm[:, 104 + gi:105 + gi], rv, ALU.mult, ALU.mult)
                        S.tt(yTv[:, 8 + gi * 4 + hd, :], yv, gz[:, 8 + gi * 4 + hd, :], ALU.mult)
                S.dma("act", yTd.rr("(kb p) t -> p kb t", p=128)[:, :, tok0:tok0 + 128], yTv)

    def phaseD(l, ffh):
        last = (l == DEPTH - 1)
        xs_ap = xsrc[l]
        xd_ap = xdst[l]
        HF = FF // 2
        W1 = S.sb("W1", [128, 8, HF], BF16)
        load_w_cast(W1, w_mlp1[l], ffh * HF, HF, 0)
        W2 = S.sb("W2", [128, 16, D_MODEL], BF16)
        load_w_cast(W2, w_mlp2[l], 0, D_MODEL, 0, row0=ffh * HF)
        if ffh == 0:
            WO = S.sb("WO", [128, 16, D_MODEL], BF16)
            load_w_cast(WO, w_out[l], 0, D_MODEL, 0)
            yTl = Rot(S, "yTl", [128, 16, 128], BF16, 2)
        h2rot = Rot(S, "h2", [128, 8, 256], BF16, 2)
        uT = S.sb("uT", [128, 16, 256], BF16)
        rl = Rot(S, "rl", [128, 256], F32, 2)
        xo = Rot(S, "xo2", [128, D_MODEL], F32, 3)
        xa = Rot(S, "xa", [128, D_MODEL], F32, 2)
        tq = S.sb("tq2", [128, 512], F32)
        if last and ffh == 1:
            fng = S.sb("fng", [128, D_MODEL], F32)
            S.dma("sp", fng, V(fng_d.ap.partition_broadcast(128), fng_d.res))
        units = range(1, NU) if last else range(NU)
        for u in units:
            tok0 = u * 256
            j = 1 if u == 0 else 0
            h2 = h2rot.next()
            bases = []
            if ffh == 0:
                for i in range(2):
                    pc = i * 128
                    xt = xrot.next()
                    S.dma("sp", xt, xs_ap[tok0 + pc:tok0 + pc + 128, :])
                    yv = yTl.next()
                    S.dma("sp", yv, yTd.rr("(kb p) t -> p kb t", p=128)[:, :, tok0 + pc:tok0 + pc + 128])
                    x1 = xo.next()
                    for half in range(2):
                        hs = slice(half * 512, (half + 1) * 512)
                        for kb in range(16):
                            S.mm(PS[2 + half], yv[:, kb, :], WO[:, kb, hs], start=(kb == 0), stop=(kb == 15))
                        S.tt(tq, PS[2 + half], GBC[:, j, 0, hs], ALU.mult)
                        S.tt(x1[:, hs], tq, xt[:, hs], ALU.add)
                    S.dma("act", xmid[tok0 + pc:tok0 + pc + 128, :], x1)
                    norm_T(x1, A2, SH2, j, h2, pc, 0)
                    bases.append(x1)
            else:
                load_norm_T(xmid, tok0, 2, A2, SH2, j, h2, 0)
                for i in range(2):
                    xb = xa.next()
                    S.dma("sp", xb, xacc[tok0 + i * 128:tok0 + (i + 1) * 128, :])
                    bases.append(xb)
            for fb in range(16):
                ps = PS[2 + fb % 2][:, 0:256]
                for kb in range(8):
                    S.mm(ps, W1[:, kb, fb * 128:(fb + 1) * 128], h2[:, kb, :], start=(kb == 0), stop=(kb == 7))
                r_ = rl.next()
                S.act(r_, ps, AF.Relu)
                S.tt(uT[:, fb, :], r_, r_, ALU.mult)
            for i in range(2):
                pc = i * 128
                xov = xa.next() if ffh == 0 else xo.next()
                for half in range(2):
                    hs = slice(half * 512, (half + 1) * 512)
                    for fb in range(16):
                        S.mm(PS[4 + half], uT[:, fb, pc:pc + 128], W2[:, fb, hs], start=(fb == 0), stop=(fb == 15))
                    S.tt(tq, PS[4 + half], GBC[:, j, 1, hs], ALU.mult)
                    S.tt(xov[:, hs], tq, bases[i][:, hs], ALU.add)
                if ffh == 0:
                    S.dma("act", xacc[tok0 + pc:tok0 + pc + 128, :], xov)
                elif last:
                    sm = small.next()
                    jk = xnrot.next()
                    S.act(jk, xov, AF.Square, accum=sm[:, 0:1])
                    S.ts(sm[:, 1:2], sm[:, 0:1], 1.0 / D_MODEL, ALU.mult, EPS, ALU.add)
                    S.act(sm[:, 2:3], sm[:, 1:2], AF.Sqrt)
                    S.recip(sm[:, 3:4], sm[:, 2:3])
                    S.stt(xov, xov, sm[:, 3:4], fng, ALU.mult, ALU.mult)
                    S.dma("act", xd_ap[tok0 - CTX + pc:tok0 - CTX + pc + 128, :], xov)
                else:
                    S.dma("act", xd_ap[tok0 + pc:tok0 + pc + 128, :], xov)

    for l in layers:
        with S.scope():
            phaseA(l)
        if doB:
            with S.scope():
                phaseB_ssd(l)
            with S.scope():
                phaseB_g(l, "gla")
            with S.scope():
                phaseB_g(l, "hg")
        if fused:
            S.barrier()
            sv = V(SsumD.ap[2:4].rearrange("a p w -> (a p) w"), SsumD.res)
            dv = V(DsegD.ap.rearrange("a p w -> (a p) w"), DsegD.res)
            groups = [list(range(NCORES))]
            for (src, dstt) in ((sv, SsumAll), (dv, DsegAll)):
                S.coll(src, dstt, groups)
            S.barrier()
        if doC:
            with S.scope():
                phaseC(l)
            with S.scope():
                phaseD(l, 0)
            with S.scope():
                phaseD(l, 1)
    S.emit()
    return nc, in_names


def host_common(inp):
    L = DEPTH
    f = lambda a: np.ascontiguousarray(np.asarray(a, dtype=np.float32))
    c = f(inp["c"])[0]
    cc = f(inp["c_ctx"])
    cvec = np.stack([c.reshape(8, 128).T, cc.reshape(8, 128).T], axis=-1)
    pfm = np.zeros((L, 128, NPF), np.float32)
    prow = np.zeros((L, NPR), np.float32)
    lbl = f(inp["hg_lb_logits"])
    for l in range(L):
        pfm[l, :, 0:8] = f(inp["norm1_g"])[l].reshape(8, 128).T
        pfm[l, :, 8:16] = f(inp["norm2_g"])[l].reshape(8, 128).T
        pfm[l, :, 16:24] = f(inp["ssd_norm_g"])[l].reshape(8, 128).T
        cw = f(inp["ssd_conv_w"])[l]
        pfm[l, :, 24:84] = cw.reshape(5, 12, 128).transpose(2, 1, 0).reshape(128, 60)
        pfm[l, :, 84:96] = f(inp["ssd_conv_b"])[l].reshape(12, 128).T
        dd = f(inp["ssd_d"])[l]
        pfm[l, :, 96:104] = np.repeat(dd.reshape(8, 2, 1), 64, axis=2).reshape(8, 128).T
        pfm[l, :, 104] = f(inp["gla_norm_g"])[l]
        pfm[l, :, 105] = f(inp["hg_norm_g"])[l]
        pfm[l, :, 106:106 + 4 * L] = lbl.reshape(L, 4, 128).transpose(2, 0, 1).reshape(128, 4 * L)
        prow[l, 0:32] = f(inp["ssd_dt_bias"])[l].reshape(32)
        prow[l, 32:64] = f(inp["ssd_a_log"])[l].reshape(32)
        prow[l, 64:576] = f(inp["gla_b_gk"])[l].reshape(512)
        prow[l, 576:576 + 512 * L] = lbl.reshape(L * 512)
        bm = f(inp["b_mod"])[l]
        prow[l, 576 + 512 * L:576 + 512 * L + 1024] = bm[2048:3072]
        prow[l, 576 + 512 * L + 1024:576 + 512 * L + 2048] = bm[5120:6144]
    wgk = np.ascontiguousarray(f(inp["gla_w_gk2"]).transpose(0, 2, 1, 3))
    bmodfm = np.ascontiguousarray(f(inp["b_mod"]).reshape(L, 48, 128).transpose(0, 2, 1))
    return dict(cvec=np.ascontiguousarray(cvec), w_mod=f(inp["w_mod"]), w_in=f(inp["w_in"]), w_out=f(inp["w_out"]),
                w_mlp1=f(inp["w_mlp1"]), w_mlp2=f(inp["w_mlp2"]), cst=make_consts(), pfm=pfm, prow=prow, wgk=wgk,
                bmodfm=bmodfm, fng=f(inp["final_norm_g"]))


def core_mask(k):
    m = np.zeros((128, 16), np.float32)
    for jj in range(NCORES):
        m[:, jj] = 1.0 if jj < k else 0.0
        m[:, 8 + jj] = 1.0 if jj > k else 0.0
    return m


SCR = ["Yssd", "Yg", "Yh", "QG", "QH", "CTd", "BTd", "SlocB", "Dall", "SsumD", "DsegD"]
_NC_CACHE = {}


def get_nc(cfg, mode):
    key = (cfg.SEQ, mode)
    if key not in _NC_CACHE:
        _NC_CACHE[key] = build(cfg, mode)
    return _NC_CACHE[key]


def run_unfused(inp, cfg):
    com = host_common(inp)
    x = np.asarray(inp["x"], np.float32)[0]
    ctx = np.asarray(inp["ctx"], np.float32)[0]
    LAT = cfg.LAT
    xs = [np.ascontiguousarray(np.concatenate([ctx, x[k * LAT:(k + 1) * LAT]], axis=0)) for k in range(NCORES)]
    out = None
    for l in range(DEPTH):
        ncB, namesB = get_nc(cfg, "B%d" % l)
        maps = [dict(com, xin=xs[k], mk=core_mask(k)) for k in range(NCORES)]
        maps = [{n: m[n] for n in namesB} for m in maps]
        rb = run_bass_kernel_spmd(ncB, maps, core_ids=list(range(NCORES))).results
        ssum = np.concatenate([np.asarray(rb[k]["SsumD"])[2:4].reshape(256, SW) for k in range(NCORES)], axis=0)
        dseg = np.concatenate([np.asarray(rb[k]["DsegD"]).reshape(256, ND) for k in range(NCORES)], axis=0)
        ncC, namesC = get_nc(cfg, "CD%d" % l)
        maps = []
        for k in range(NCORES):
            m = dict(com, xin=xs[k], mk=core_mask(k), SsumAll=ssum, DsegAll=dseg)
            for nm in SCR:
                m[nm] = np.asarray(rb[k][nm])
            maps.append({n: m[n] for n in namesC})
        rc = run_bass_kernel_spmd(ncC, maps, core_ids=list(range(NCORES))).results
        if l < DEPTH - 1:
            xs = [np.asarray(rc[k]["xout"]) for k in range(NCORES)]
        else:
            out = np.concatenate([np.asarray(rc[k]["xout"]) for k in range(NCORES)], axis=0)
    return out[None].astype(np.float32)


def run_fused(inp, cfg):
    com = host_common(inp)
    x = np.asarray(inp["x"], np.float32)[0]
    ctx = np.asarray(inp["ctx"], np.float32)[0]
    LAT = cfg.LAT
    nc, names = get_nc(cfg, "ALL")
    maps = [dict(com, xin=np.ascontiguousarray(np.concatenate([ctx, x[k * LAT:(k + 1) * LAT]], axis=0)), mk=core_mask(k))
            for k in range(NCORES)]
    maps = [{n: m[n] for n in names} for m in maps]
    r = run_bass_kernel_spmd(nc, maps, core_ids=list(range(NCORES))).results
    out = np.concatenate([np.asarray(r[k]["out"]) for k in range(NCORES)], axis=0)
    return out[None].astype(np.float32)


FUSED = False
DEBUG_YT = False


def kernel(**inputs):
    cfg = Cfg(np.asarray(inputs["x"]).shape[1])
    if FUSED:
        return run_fused(inputs, cfg)
    return run_unfused(inputs, cfg)
```
